# Optimizing a Trainium2 kernel written in Bass

```python
import jax, jax.numpy as jnp
from jax import lax
import numpy as np

D_MODEL = 1024
BATCH = 8
SEQ = 2048
DEPTH = 4

EXPAND = 2
E_INNER = EXPAND * D_MODEL
HEAD_DIM = 128
E_A = E_INNER // 2
E_B = E_INNER - E_A
H_A = E_A // HEAD_DIM
H_B = E_B // HEAD_DIM
CONV_WIDTH = 3
CHUNK = 128
AB_SPLITS = (E_A, E_A, E_A, E_A, E_B, E_B, E_B)
AB_IN = sum(AB_SPLITS)
E_C = E_INNER
POOL_WINDOWS = (2, 4, 8, 16)
N_POOL_GROUPS = len(POOL_WINDOWS)
G_C = E_C // N_POOL_GROUPS
N_EVEN = (DEPTH + 1) // 2
N_ODD = DEPTH // 2
EPS = 1e-6

kernel_name = "hybrid_shortconv_sgu_pool_adaln_trunk"


def rmsnorm(x, g):
    x32 = x.astype(jnp.float32)
    y = x32 * lax.rsqrt(jnp.mean(x32 * x32, axis=-1, keepdims=True) + EPS)
    return (y * g.astype(jnp.float32)).astype(x.dtype)


def modulate(h, shift, scale):
    return h * (1 + scale[:, None, :]) + shift[:, None, :]


def causal_short_conv(x, w):
    S = x.shape[1]
    xp = jnp.pad(x, ((0, 0), (CONV_WIDTH - 1, 0), (0, 0)))
    y = xp[:, 0:S] * w[0]
    for k in range(1, CONV_WIDTH):
        y = y + xp[:, k:k + S] * w[k]
    return y


def chunked_sgu(u, v, ln_g, ln_b, w_s, b_s):
    Bn, S, _ = v.shape
    n_chunks = S // CHUNK
    v32 = v.astype(jnp.float32).reshape(Bn, S, H_B, HEAD_DIM)
    mu = jnp.mean(v32, axis=-1, keepdims=True)
    var = jnp.mean(jnp.square(v32 - mu), axis=-1, keepdims=True)
    vn = ((v32 - mu) * lax.rsqrt(var + EPS)).reshape(Bn, S, E_B)
    vn = (vn * ln_g.astype(jnp.float32) + ln_b.astype(jnp.float32)).astype(v.dtype)
    vn = vn.reshape(Bn, n_chunks, CHUNK, H_B, HEAD_DIM)
    causal = jnp.tril(jnp.ones((CHUNK, CHUNK), dtype=bool))
    w_masked = jnp.where(causal[None], w_s, jnp.zeros_like(w_s))
    mixed = jnp.einsum('hts,bnshd->bnthd', w_masked, vn) + b_s.T[None, None, :, :, None]
    return u * mixed.reshape(Bn, S, E_B)


def multiscale_pool(p):
    S = p.shape[1]
    p32 = p.astype(jnp.float32)
    cs = jnp.cumsum(p32, axis=1)
    outs = []
    for gi, win in enumerate(POOL_WINDOWS):
        sl = slice(gi * G_C, (gi + 1) * G_C)
        csg = cs[..., sl]
        prev = jnp.pad(csg, ((0, 0), (win, 0), (0, 0)))[:, :S]
        cnt = jnp.minimum(jnp.arange(1, S + 1), win).astype(jnp.float32)[None, :, None]
        outs.append((csg - prev) / cnt - p32[..., sl])
    return jnp.stack(outs, axis=2).astype(p.dtype)


def even_mixer(h, w_in, conv_w, ln_g, ln_b, w_s, b_s, w_out):
    proj = h @ w_in
    idx = [int(i) for i in np.cumsum(AB_SPLITS)[:-1]]
    a_h, a_b, a_c, a_z, b_u, b_v, b_z = jnp.split(proj, idx, axis=-1)
    y_a = a_b * causal_short_conv(a_c * a_h, conv_w)
    y_a = y_a * jax.nn.silu(a_z)
    y_b = chunked_sgu(b_u, b_v, ln_g, ln_b, w_s, b_s)
    y_b = y_b * jax.nn.silu(b_z)
    return jnp.concatenate([y_a, y_b], axis=-1) @ w_out


def odd_mixer(h, w_in, pool_w, pool_scale, w_out):
    Bn, S, _ = h.shape
    proj = h @ w_in
    p, z = jnp.split(proj, 2, axis=-1)
    pooled = multiscale_pool(p)
    y = jnp.einsum('bsgi,gio->bsgo', pooled, pool_w).reshape(Bn, S, E_C)
    y = y * pool_scale * jax.nn.silu(z)
    return y @ w_out


def setup_inputs(seed: int = 0) -> dict:
    key = jax.random.key(seed)
    ks = jax.random.split(key, 20)
    nrm = jax.random.normal
    f32 = jnp.float32
    return {
        "x": nrm(ks[0], (BATCH, SEQ, D_MODEL), f32),
        "c": nrm(ks[1], (BATCH, D_MODEL), f32),
        "norm_g": 1.0 + 0.1 * nrm(ks[2], (DEPTH, D_MODEL), f32),
        "ada_w": nrm(ks[3], (DEPTH, D_MODEL, 3 * D_MODEL), f32) * D_MODEL ** -0.5,
        "ada_b": 0.01 * nrm(ks[4], (DEPTH, 3 * D_MODEL), f32),
        "ab_w_in": nrm(ks[5], (N_EVEN, D_MODEL, AB_IN), f32) * D_MODEL ** -0.5,
        "ab_conv_w": nrm(ks[6], (N_EVEN, CONV_WIDTH, E_A), f32) * CONV_WIDTH ** -0.5,
        "ab_ln_g": 1.0 + 0.1 * nrm(ks[7], (N_EVEN, E_B), f32),
        "ab_ln_b": 0.02 * nrm(ks[8], (N_EVEN, E_B), f32),
        "ab_sgu_w": nrm(ks[9], (N_EVEN, H_B, CHUNK, CHUNK), f32) * CHUNK ** -0.5,
        "ab_sgu_b": 1.0 + 0.1 * nrm(ks[10], (N_EVEN, H_B, CHUNK), f32),
        "ab_w_out": nrm(ks[11], (N_EVEN, E_A + E_B, D_MODEL), f32) * (E_A + E_B) ** -0.5,
        "c_w_in": nrm(ks[12], (N_ODD, D_MODEL, 2 * E_C), f32) * D_MODEL ** -0.5,
        "c_pool_w": nrm(ks[13], (N_ODD, N_POOL_GROUPS, G_C, G_C), f32) * G_C ** -0.5,
        "c_pool_scale": 1.0 + 0.1 * nrm(ks[14], (N_ODD, E_C), f32),
        "c_w_out": nrm(ks[15], (N_ODD, E_C, D_MODEL), f32) * E_C ** -0.5,
        "final_g": 1.0 + 0.1 * nrm(ks[16], (D_MODEL,), f32),
    }


def reference(x, c, norm_g, ada_w, ada_b, ab_w_in, ab_conv_w, ab_ln_g, ab_ln_b,
              ab_sgu_w, ab_sgu_b, ab_w_out, c_w_in, c_pool_w, c_pool_scale,
              c_w_out, final_g):
    c_act = jax.nn.silu(c)
    for i in range(DEPTH):
        mod = c_act @ ada_w[i] + ada_b[i]
        shift, scale, gate = jnp.split(mod, 3, axis=-1)
        h = modulate(rmsnorm(x, norm_g[i]), shift, scale)
        j = i // 2
        if i % 2 == 0:
            out = even_mixer(h, ab_w_in[j], ab_conv_w[j], ab_ln_g[j], ab_ln_b[j],
                             ab_sgu_w[j], ab_sgu_b[j], ab_w_out[j])
        else:
            out = odd_mixer(h, c_w_in[j], c_pool_w[j], c_pool_scale[j], c_w_out[j])
        x = x + gate[:, None, :] * out
    return rmsnorm(x, final_g)
```

```python
import numpy as np
from contextlib import ExitStack
import concourse.bass as bass
import concourse.mybir as mybir
from concourse.bass_utils import run_bass_kernel_spmd

F32 = mybir.dt.float32
BF16 = mybir.dt.bfloat16
AF = mybir.ActivationFunctionType
ALU = mybir.AluOpType
AX = mybir.AxisListType

D = 1024
S = 2048
TT = 512
NT = S // TT
KC = D // 128
DEPTH = 4
EPS = 1e-6
NS = 16
NF = 12
NBQ = 12
FW = 528
NOUTSEM = 16

V_NG = 0
V_FG = 32
V_AB = 40
V_CW = 136
V_LG = 184
V_LB = 200
V_PS = 216
NV = 248

FUSED = True


class Buf:
    __slots__ = ("ap", "w_ev", "r_evs", "excl", "sem")

    def __init__(self, ap=None, excl=False, sem=None):
        self.ap = ap
        self.w_ev = None
        self.r_evs = {}
        self.excl = excl
        self.sem = sem


class Prog:
    ENG = ("pe", "act", "dve", "pool", "sp")

    def __init__(self):
        self.q = {k: [] for k in self.ENG}
        self.cnt = {}
        self.seen = {k: {} for k in self.ENG}
        self.sems = {}
        self.semkeys = list(self.ENG)

    def new_sem(self, key):
        assert key not in self.semkeys
        self.semkeys.append(key)
        return key

    def wait(self, eng, ev):
        if ev is None:
            return
        key, v = ev
        if eng == "pe" and key == "pe":
            return
        if v > self.seen[eng].get(key, 0):
            self.seen[eng][key] = v
            self.q[eng].append(("wait", key, v))

    def _deps(self, eng, reads, writes):
        for b in reads:
            self.wait(eng, b.w_ev)
            if b.excl:
                for k, v in b.r_evs.items():
                    if k != eng:
                        self.wait(eng, (k, v))
        for b in writes:
            self.wait(eng, b.w_ev)
            for k, v in b.r_evs.items():
                if k != eng:
                    self.wait(eng, (k, v))

    def _commit(self, ev, reads, writes):
        for b in reads:
            if ev[1] > b.r_evs.get(ev[0], 0):
                b.r_evs[ev[0]] = ev[1]
        for b in writes:
            b.w_ev = ev
            b.r_evs = {}

    def op(self, eng, fn, reads=(), writes=()):
        self._deps(eng, reads, writes)
        self.cnt[eng] = self.cnt.get(eng, 0) + 1
        ev = (eng, self.cnt[eng])
        self.q[eng].append(("op", fn, eng, 1))
        self._commit(ev, reads, writes)
        return ev

    def pe_group(self, fns, reads=(), writes=()):
        self._deps("pe", reads, writes)
        for fn in fns[:-1]:
            self.q["pe"].append(("op", fn, None, 0))
        self.cnt["pe"] = self.cnt.get("pe", 0) + 1
        ev = ("pe", self.cnt["pe"])
        self.q["pe"].append(("op", fns[-1], "pe", 1))
        self._commit(ev, reads, writes)
        return ev

    def dma(self, eng, fn, sem, reads=(), writes=()):
        self._deps(eng, reads, writes)
        self.cnt[sem] = self.cnt.get(sem, 0) + 16
        ev = (sem, self.cnt[sem])
        self.q[eng].append(("op", fn, sem, 16))
        self._commit(ev, reads, writes)
        return ev

    def replay(self, eng, e):
        for it in self.q[eng]:
            if it[0] == "wait":
                e.wait_ge(self.sems[it[1]], it[2])
            else:
                ins = it[1](e)
                if it[2] is not None:
                    assert ins is not None
                    ins.then_inc(self.sems[it[2]], it[3])


class Emitter:
    def __init__(self, nc, es, layers, do_final):
        self.nc = nc
        self.es = es
        self.layers = layers
        self.do_final = do_final
        self.P = Prog()
        self.descs = []
        self.loaded = set()
        self.n_acq = 0
        self._bank_i = 0
        self._f_i = 0
        self._bq_i = 0
        self._out_i = 0

    def sb(self, name, shape, dt):
        return self.es.enter_context(self.nc.sbuf_tensor(name, shape, dt))

    def setup_mem(self):
        nc, P = self.nc, self.P
        self.xT = nc.dram_tensor("xT", [D, S], F32, kind="ExternalInput").ap()
        self.cvec = nc.dram_tensor("cvec", [128, KC], F32, kind="ExternalInput").ap()
        self.vecs_d = nc.dram_tensor("vecs", [128, NV], F32, kind="ExternalInput").ap()
        self.sgub_d = nc.dram_tensor("sgub", [2, 1024], F32, kind="ExternalInput").ap()
        self.sguT_d = nc.dram_tensor("sguT", [2, 128, 1024], F32, kind="ExternalInput").ap()
        self.outT = nc.dram_tensor("outT", [D, S], F32, kind="ExternalOutput").ap()

        self.Xt = self.sb("X", [128, KC, S], F32)
        self.Ht = self.sb("H", [128, KC, S], BF16)
        self.Yt = self.sb("Y", [128, 4, S], BF16)
        self.ringt = self.sb("ring", [128, NS, 1024], BF16)
        self.Ft = self.sb("F", [128, NF, FW], F32)
        self.BQt = self.sb("BQ", [128, NBQ, TT], BF16)
        self.VHt = self.sb("VH", [128, 2, 16, 128], BF16)
        self.CTt = self.sb("CT", [128, 8, 128], F32)
        self.WMTt = self.sb("WMT", [128, 8, 128], BF16)
        self.vecs = self.sb("vecs_s", [128, NV], F32)
        self.cin = self.sb("cin", [128, KC], F32)
        self.cact = self.sb("cact", [128, KC], BF16)
        self.modv = self.sb("modv", [128, DEPTH * 24], F32)
        self.gsv = self.sb("gsv", [128, DEPTH * 8], F32)
        self.ones = self.sb("ones", [128, 128], BF16)
        self.mhalf = self.sb("mhalf", [128, 1], F32)
        self.onesf = self.sb("onesf", [128, 128], F32)
        self.ident = self.sb("ident", [128, 128], F32)
        self.nrm = self.sb("nrm", [128, 4, 2, 4], F32)
        self.diagt = self.sb("diag", [128, 2, 4, 128], F32)
        self.invc = self.sb("invc", [128, 4, 16], F32)
        self.halo = self.sb("halo", [128, 4, 16], F32)
        self.stat = self.sb("stat", [128, 2, 7, 16], F32)
        self.fix = self.sb("fix", [128, 16], F32)

        self.banks = []
        for i in range(8):
            t = self.es.enter_context(nc.psum_tensor(f"ps{i}", [128, TT], F32))
            self.banks.append(Buf(ap=t, excl=True))

        self.X = [[Buf(ap=self.Xt[:, k, tt * TT:(tt + 1) * TT]) for tt in range(NT)] for k in range(KC)]
        self.H = [[Buf(ap=self.Ht[:, k, tt * TT:(tt + 1) * TT]) for tt in range(NT)] for k in range(KC)]
        self.Y = [[Buf(ap=self.Yt[:, k, tt * TT:(tt + 1) * TT]) for tt in range(NT)] for k in range(4)]
        self.slots = [Buf(ap=self.ringt[:, i, :], sem=P.new_sem(f"rg{i}")) for i in range(NS)]
        self.F = [Buf(ap=self.Ft[:, i, :]) for i in range(NF)]
        self.BQ = [Buf(ap=self.BQt[:, i, :]) for i in range(NBQ)]
        self.VH = [Buf(), Buf()]
        self.CT = Buf()
        self.WMT = Buf()
        self.vecsb = Buf()
        self.cactb = Buf()
        self.cinb = Buf()
        self.modb = [Buf() for _ in range(DEPTH)]
        self.constb = Buf()
        self.halob = [Buf() for _ in range(4)]
        self.statb = [[Buf() for _ in range(7)] for _ in range(2)]
        self.fixb = Buf()
        self.nrmb = [(Buf(), Buf()) for _ in range(4)]
        self._nrm_i = 0
        self.diagb = [[Buf(ap=self.diagt[:, p, jj, :]) for jj in range(4)] for p in range(2)]
        self.xsem = [P.new_sem(f"xs{k}") for k in range(KC)]
        self.outsem = [P.new_sem(f"os{k}") for k in range(NOUTSEM)]
        self.msem = [P.new_sem(f"ms{k}") for k in range(4)]

    def bank(self):
        b = self.banks[self._bank_i % 8]
        self._bank_i += 1
        return b

    def ftile(self):
        b = self.F[self._f_i % NF]
        self._f_i += 1
        return b

    def bqtile(self):
        b = self.BQ[self._bq_i % NBQ]
        self._bq_i += 1
        return b

    def _ring_load(self, idx):
        slot = self.slots[idx % NS]
        self.loaded.add(idx)

        def fn(e, idx=idx, slot=slot):
            if idx >= len(self.descs):
                return None
            return e.dma_start(out=slot.ap, in_=self.wblk[idx])
        self.P._deps("pool", (), (slot,))
        P = self.P
        P.cnt[slot.sem] = P.cnt.get(slot.sem, 0) + 16
        ev = (slot.sem, P.cnt[slot.sem])
        P.q["pool"].append(("dma_lazy", fn, slot.sem, 16))
        P._commit(ev, (), (slot,))

    def ring_start(self):
        for i in range(NS):
            self._ring_load(i)

    def acquire(self, desc):
        idx = self.n_acq
        self.n_acq += 1
        assert idx in self.loaded, f"ring too small at block {idx} {desc}"
        self.descs.append(desc)
        return idx

    def release(self, idx):
        self._ring_load(idx + NS)

    def slot(self, idx):
        return self.slots[idx % NS]

    def act(self, out, in_, func, reads, writes, scale=None, bias=None):
        kw = {}
        if scale is not None:
            kw["scale"] = scale
        if bias is not None:
            kw["bias"] = bias
        return self.P.op("act", lambda e: e.activation(out=out, in_=in_, func=func, **kw), reads, writes)

    def tt_(self, eng, out, in0, in1, op, reads, writes):
        return self.P.op(eng, lambda e: e.tensor_tensor(out=out, in0=in0, in1=in1, op=op), reads, writes)

    def ts_(self, out, in0, s1, s2, op0, op1, reads, writes):
        if s2 is None:
            return self.P.op("dve", lambda e: e.tensor_scalar(out=out, in0=in0, scalar1=s1, scalar2=None, op0=op0), reads, writes)
        return self.P.op("dve", lambda e: e.tensor_scalar(out=out, in0=in0, scalar1=s1, scalar2=s2, op0=op0, op1=op1), reads, writes)

    def stt(self, out, in0, scalar, in1, op0, op1, reads, writes):
        return self.P.op("dve", lambda e: e.scalar_tensor_tensor(out=out, in0=in0, scalar=scalar, in1=in1, op0=op0, op1=op1), reads, writes)

    def memset(self, eng, ap, val, writes):
        return self.P.op(eng, lambda e: e.memset(ap, val), (), writes)

    def mm_group(self, bankbuf, out_ap, pairs, reads):
        n = len(pairs)
        fns = []
        for i, (l, r) in enumerate(pairs):
            fns.append(lambda e, l=l, r=r, i=i: e.matmul(out_ap, l, r, start=(i == 0), stop=(i == n - 1)))
        return self.P.pe_group(fns, reads, [bankbuf])

    def emit_setup(self):
        P = self.P
        self.memset("dve", self.ones[:], 1.0, [self.constb])
        self.memset("dve", self.mhalf[:], -0.5, [self.constb])
        self.memset("dve", self.onesf[:], 1.0, [self.constb])
        self.memset("pool", self.ident[:], 1.0, [self.constb])
        self.P.op("pool", lambda e: e.affine_select(out=self.ident[:], in_=self.ident[:], pattern=[[1, 128]],
                                                     compare_op=ALU.is_equal, fill=0.0, base=0, channel_multiplier=-1),
                  [self.constb], [self.constb])
        for g in range(4):
            W = 2 << g
            self.memset("dve", self.invc[:, g, :], 1.0 / W, [self.constb])
            for t in range(W - 1):
                self.memset("dve", self.invc[:, g, t:t + 1], 1.0 / (t + 1), [self.constb])
        P.dma("sp", lambda e: e.dma_start(out=self.vecs[:], in_=self.vecs_d), self.msem[0], (), [self.vecsb])
        P.dma("sp", lambda e: e.dma_start(out=self.cin[:], in_=self.cvec), self.msem[1], (), [self.cinb])
        for k in range(KC):
            P.dma("sp", lambda e, k=k: e.dma_start(out=self.Xt[:, k, :], in_=self.xT[k * 128:(k + 1) * 128, :]),
                  self.xsem[k], (), self.X[k])
        self.act(self.cact[:], self.cin[:], AF.Silu, [self.cinb], [self.cactb])
        self.ring_start()

    def emit_mod_part(self, l, mc0, mc1, st):
        bk = self.bank()
        mb = self.modb[l]
        for mc in range(mc0, mc1):
            idx = self.acquire(("ada", l, mc))
            sl = self.slot(idx)
            pairs = [(sl.ap[:, k * 128:(k + 1) * 128], self.cact[:, k:k + 1]) for k in range(KC)]
            self.mm_group(bk, bk.ap[:, mc:mc + 1], pairs, [sl, self.cactb])
            self.release(idx)
        self.tt_("dve", self.modv[:, l * 24 + mc0:l * 24 + mc1], bk.ap[:, mc0:mc1],
                 self.vecs[:, V_AB + l * 24 + mc0:V_AB + l * 24 + mc1], ALU.add, [bk, self.vecsb], [mb])
        if mc0 < 16 <= mc1:
            self.stt(self.gsv[:, l * 8:(l + 1) * 8], self.modv[:, l * 24 + 8:l * 24 + 16], 1.0,
                     self.vecs[:, V_NG + l * 8:V_NG + (l + 1) * 8], ALU.add, ALU.mult, [mb, self.vecsb], [mb])

    def shift(self, l, k):
        return self.modv[:, l * 24 + k:l * 24 + k + 1]

    def gate(self, l, k):
        return self.modv[:, l * 24 + 16 + k:l * 24 + 16 + k + 1]

    def gs(self, l, k):
        return self.gsv[:, l * 8 + k:l * 8 + k + 1]

    def norm_a1(self, tt):
        sq = []
        for k in range(KC):
            q = self.bqtile()
            self.act(q.ap, self.X[k][tt].ap, AF.Square, [self.X[k][tt]], [q])
            sq.append(q)
        bk = self.bank()
        fns = []
        for jj in range(4):
            for k in range(KC):
                lhsT = sq[k].ap[:, jj * 128:(jj + 1) * 128]
                out = bk.ap[:, jj:jj + 1]
                fns.append(lambda e, out=out, lhsT=lhsT, k=k: e.matmul(out, lhsT, self.ones[:, 0:1], start=(k == 0), stop=(k == KC - 1)))
        self.P.pe_group(fns, sq + [self.constb], [bk])
        ns = self._nrm_i % 4
        self._nrm_i += 1
        tb, rb = self.nrmb[ns]
        tap = self.nrm[:, ns, 0, :]
        rap = self.nrm[:, ns, 1, :]
        self.ts_(tap, bk.ap[:, 0:4], 1.0 / D, EPS, ALU.mult, ALU.add, [bk], [tb])
        self.tt_("pool", rap, tap, self.mhalf[:, 0:1].to_broadcast([128, 4]), ALU.pow, [tb, self.constb], [rb])
        return ns

    def norm_a2(self, ns):
        par = ns % 2
        rb = self.nrmb[ns][1]
        diags = []
        for jj in range(4):
            dg = self.diagb[par][jj]
            self.ts_(dg.ap, self.ident[:], self.nrm[:, ns, 1, jj:jj + 1], None, ALU.mult, None,
                     [rb, self.constb], [dg])
            diags.append(dg)
        return diags

    def norm_a(self, tt):
        return self.norm_a2(self.norm_a1(tt))

    def norm_r(self, diags):
        bk = self.bank()
        fns = []
        for jj in range(4):
            out = bk.ap[:, jj * 128:(jj + 1) * 128]
            rhs = diags[jj].ap
            fns.append(lambda e, out=out, rhs=rhs: e.matmul(out, self.onesf[:], rhs, start=True, stop=True))
        self.P.pe_group(fns, diags + [self.constb], [bk])
        return bk

    def norm_b(self, l, tt, diags):
        rk = self.norm_r(diags)
        mb = self.modb[l]
        for k in range(KC):
            xn = self.ftile()
            self.tt_("dve", xn.ap[:, 0:TT], rk.ap[:, :], self.X[k][tt].ap, ALU.mult, [self.X[k][tt], rk], [xn])
            self.act(self.H[k][tt].ap, xn.ap[:, 0:TT], AF.Identity, [xn, mb], [self.H[k][tt]],
                     scale=self.gs(l, k), bias=self.shift(l, k))

    def final_b(self, tt, diags):
        rk = self.norm_r(diags)
        for k in range(KC):
            o = self.ftile()
            self.stt(o.ap[:, 0:TT], self.X[k][tt].ap, self.vecs[:, V_FG + k:V_FG + k + 1], rk.ap[:, :],
                     ALU.mult, ALU.mult, [self.X[k][tt], rk, self.vecsb], [o])
            sem = self.outsem[self._out_i % NOUTSEM]
            self._out_i += 1
            self.P.dma("sp", lambda e, o=o, k=k, tt=tt: e.dma_start(
                out=self.outT[k * 128:(k + 1) * 128, tt * TT:(tt + 1) * TT], in_=o.ap[:, 0:TT]), sem, [o], ())

    def norm_pipeline(self, l):
        st = {}

        def bfn(t):
            if l is None:
                self.final_b(t, st[t])
            else:
                self.norm_b(l, t, st[t])

        def cb(tt):
            if tt == NT:
                st[NT - 1] = self.norm_a(NT - 1)
                bfn(NT - 2)
                bfn(NT - 1)
            else:
                if tt >= 1:
                    st[tt - 1] = self.norm_a(tt - 1)
                if tt >= 2:
                    bfn(tt - 2)
        return cb

    def emit_store_x(self):
        for k in range(KC):
            sem = self.outsem[self._out_i % NOUTSEM]
            self._out_i += 1
            self.P.dma("sp", lambda e, k=k: e.dma_start(out=self.outT[k * 128:(k + 1) * 128, :], in_=self.Xt[:, k, :]),
                       sem, self.X[k], ())

    def proj_fm(self, idx, tt):
        sl = self.slot(idx)
        bk = self.bank()
        pairs = [(sl.ap[:, k * 128:(k + 1) * 128], self.H[k][tt].ap) for k in range(KC)]
        self.mm_group(bk, bk.ap[:, :], pairs, [sl] + [self.H[k][tt] for k in range(KC)])
        return bk

    def emit_outproj(self, l, arr, j, row0, cb=None):
        idxs = [self.acquire(("wout", arr, j, row0, mp)) for mp in range(4)]
        mb = self.modb[l]
        for tt in range(NT):
            for m in range(KC):
                sl = self.slot(idxs[m // 2])
                bk = self.bank()
                pairs = [(sl.ap[:, ((m % 2) * 4 + kk) * 128:((m % 2) * 4 + kk + 1) * 128], self.Y[kk][tt].ap)
                         for kk in range(4)]
                self.mm_group(bk, bk.ap[:, :], pairs, [sl] + [self.Y[kk][tt] for kk in range(4)])
                xb = self.X[m][tt]
                self.stt(xb.ap, bk.ap[:, :], self.gate(l, m), xb.ap, ALU.mult, ALU.add, [bk, xb, mb], [xb])
            if cb is not None:
                cb(tt)
        for i in idxs:
            self.release(i)
        if cb is not None:
            cb(NT)

    def emit_even_A(self, l, j, c0):
        for cl in range(4):
            c = c0 + cl
            iz = self.acquire(("win_e", j, 3 * 1024 + c * 128))
            ih = self.acquire(("win_e", j, 0 * 1024 + c * 128))
            iC = self.acquire(("win_e", j, 2 * 1024 + c * 128))
            iB = self.acquire(("win_e", j, 1 * 1024 + c * 128))
            w0 = self.vecs[:, V_CW + (j * 3 + 0) * 8 + c:V_CW + (j * 3 + 0) * 8 + c + 1]
            w1 = self.vecs[:, V_CW + (j * 3 + 1) * 8 + c:V_CW + (j * 3 + 1) * 8 + c + 1]
            w2 = self.vecs[:, V_CW + (j * 3 + 2) * 8 + c:V_CW + (j * 3 + 2) * 8 + c + 1]
            prev = None
            for tt in range(NT):
                pz = self.proj_fm(iz, tt)
                ph = self.proj_fm(ih, tt)
                pC = self.proj_fm(iC, tt)
                pB = self.proj_fm(iB, tt)
                sz = self.ftile()
                self.act(sz.ap[:, 0:TT], pz.ap[:, :], AF.Silu, [pz], [sz])
                ah = self.ftile()
                self.act(ah.ap[:, 0:TT], ph.ap[:, :], AF.Copy, [ph], [ah])
                ch = self.ftile()
                self.tt_("dve", ch.ap[:, 16:FW], pC.ap[:, :], ah.ap[:, 0:TT], ALU.mult, [pC, ah], [ch])
                if tt == 0:
                    self.memset("dve", ch.ap[:, 14:16], 0.0, [ch])
                else:
                    self.P.op("dve", lambda e, ch=ch, prev=prev: e.tensor_copy(out=ch.ap[:, 14:16], in_=prev.ap[:, FW - 2:FW]),
                              [prev], [ch])
                t0 = self.ftile()
                self.act(t0.ap[:, 0:TT], ch.ap[:, 16:FW], AF.Copy, [ch, self.vecsb], [t0], scale=w2)
                t1 = self.ftile()
                self.stt(t1.ap[:, 0:TT], ch.ap[:, 15:FW - 1], w1, t0.ap[:, 0:TT], ALU.mult, ALU.add, [ch, t0, self.vecsb], [t1])
                t2 = self.ftile()
                self.stt(t2.ap[:, 0:TT], ch.ap[:, 14:FW - 2], w0, t1.ap[:, 0:TT], ALU.mult, ALU.add, [ch, t1, self.vecsb], [t2])
                u = self.ftile()
                self.tt_("dve", u.ap[:, 0:TT], pB.ap[:, :], t2.ap[:, 0:TT], ALU.mult, [pB, t2], [u])
                yb = self.Y[cl][tt]
                self.tt_("dve", yb.ap, u.ap[:, 0:TT], sz.ap[:, 0:TT], ALU.mult, [u, sz], [yb])
                prev = ch
            for i in (iz, ih, iC, iB):
                self.release(i)

    def emit_sgu_loads(self, j):
        P = self.P
        P.dma("pool", lambda e: e.dma_start(out=self.WMTt[:].rearrange("p a b -> p (a b)"), in_=self.sguT_d[j]),
              self.msem[2], (), [self.WMT])
        P.op("pool", lambda e: e.affine_select(out=self.WMTt[:], in_=self.WMTt[:], pattern=[[0, 8], [1, 128]],
                                                compare_op=ALU.is_ge, fill=0.0, base=0, channel_multiplier=-1),
             [self.WMT], [self.WMT])
        P.dma("sp", lambda e: e.dma_start(out=self.CTt[:].rearrange("p a b -> p (a b)"),
                                          in_=self.sgub_d[j:j + 1, :].to_broadcast([128, 1024])),
              self.msem[3], (), [self.CT])

    def emit_sgu_consts(self, j):
        for half in range(2):
            bk = self.bank()
            rhs = self.WMTt[:, half * 4:(half + 1) * 4, :].rearrange("p a b -> p (a b)")
            self.mm_group(bk, bk.ap[:, :], [(self.ones[:], rhs)], [self.WMT, self.constb])
            for hh in range(4):
                c = half * 4 + hh
                lnb = self.vecs[:, V_LB + j * 8 + c:V_LB + j * 8 + c + 1]
                self.stt(self.CTt[:, c, :], bk.ap[:, hh * 128:(hh + 1) * 128], lnb, self.CTt[:, c, :],
                         ALU.mult, ALU.add, [bk, self.CT, self.vecsb], [self.CT])

    def emit_sgu_stage1(self, j, c):
        par = c % 2
        iv = self.acquire(("win_e", j, 5 * 1024 + c * 128))
        sl = self.slot(iv)
        st = self.stat
        sb_ = self.statb[par]
        vts = []
        for bi in range(NT):
            bk = self.bank()
            fns = []
            for jj in range(4):
                for k in range(KC):
                    lhsT = self.Ht[:, k, bi * TT + jj * 128:bi * TT + (jj + 1) * 128]
                    rhs = sl.ap[:, k * 128:(k + 1) * 128]
                    out = bk.ap[:, jj * 128:(jj + 1) * 128]
                    fns.append(lambda e, out=out, lhsT=lhsT, rhs=rhs, k=k: e.matmul(out, lhsT, rhs, start=(k == 0), stop=(k == KC - 1)))
            self.P.pe_group(fns, [sl] + [self.H[k][bi] for k in range(KC)], [bk])
            vs = self.ftile()
            self.act(vs.ap[:, 0:TT], bk.ap[:, :], AF.Copy, [bk], [vs])
            sq = self.ftile()
            self.act(sq.ap[:, 0:TT], bk.ap[:, :], AF.Square, [bk], [sq])
            self.P.op("dve", lambda e, vs=vs, bi=bi: e.tensor_reduce(
                out=st[:, par, 0, bi * 4:(bi + 1) * 4], in_=vs.ap[:, 0:TT].rearrange("p (a b) -> p a b", a=4),
                axis=AX.X, op=ALU.add), [vs], [sb_[0]])
            self.P.op("dve", lambda e, sq=sq, bi=bi: e.tensor_reduce(
                out=st[:, par, 1, bi * 4:(bi + 1) * 4], in_=sq.ap[:, 0:TT].rearrange("p (a b) -> p a b", a=4),
                axis=AX.X, op=ALU.add), [sq], [sb_[1]])
            vts.append(vs)
        self.release(iv)
        SUM, SSQ, MEAN, MSQ, T_, RSTD, NMR = [st[:, par, i, :] for i in range(7)]
        self.ts_(MEAN, SUM, 1.0 / 128, None, ALU.mult, None, [sb_[0]], [sb_[2]])
        self.tt_("dve", MSQ, MEAN, MEAN, ALU.mult, [sb_[2]], [sb_[3]])
        self.stt(T_, SSQ, 1.0 / 128, MSQ, ALU.mult, ALU.subtract, [sb_[1], sb_[3]], [sb_[4]])
        self.ts_(T_, T_, EPS, None, ALU.add, None, [sb_[4]], [sb_[4]])
        self.tt_("pool", RSTD, T_, self.mhalf[:, 0:1].to_broadcast([128, 16]), ALU.pow, [sb_[4], self.constb], [sb_[5]])
        self.stt(NMR, MEAN, -1.0, RSTD, ALU.mult, ALU.mult, [sb_[2], sb_[5]], [sb_[6]])

        def vhat(bi):
            for jj in range(4):
                jx = bi * 4 + jj
                self.act(self.VHt[:, par, jx, :], vts[bi].ap[:, jj * 128:(jj + 1) * 128], AF.Identity,
                         [vts[bi], sb_[5], sb_[6]], [self.VH[par]],
                         scale=st[:, par, 5, jx:jx + 1], bias=st[:, par, 6, jx:jx + 1])
        return vhat

    def emit_sgu_stage2(self, j, c, cl, next_vhat=None):
        par = c % 2
        iu = self.acquire(("win_e", j, 4 * 1024 + c * 128))
        iz = self.acquire(("win_e", j, 6 * 1024 + c * 128))
        lng = self.vecs[:, V_LG + j * 8 + c:V_LG + j * 8 + c + 1]
        for tt in range(NT):
            if next_vhat is not None:
                next_vhat(tt)
            pm = self.bank()
            fns = []
            for jj in range(4):
                lhsT = self.VHt[:, par, tt * 4 + jj, :]
                rhs = self.WMTt[:, c, :]
                out = pm.ap[:, jj * 128:(jj + 1) * 128]
                fns.append(lambda e, out=out, lhsT=lhsT, rhs=rhs: e.matmul(out, lhsT, rhs, start=True, stop=True))
            self.P.pe_group(fns, [self.VH[par], self.WMT], [pm])
            pu = self.proj_fm(iu, tt)
            pz = self.proj_fm(iz, tt)
            sz = self.ftile()
            self.act(sz.ap[:, 0:TT], pz.ap[:, :], AF.Silu, [pz], [sz])
            m1 = self.ftile()
            self.stt(m1.ap[:, 0:TT].rearrange("p (a b) -> p a b", a=4), pm.ap[:, :].rearrange("p (a b) -> p a b", a=4),
                     lng, self.CTt[:, c, :].unsqueeze(1).to_broadcast([128, 4, 128]), ALU.mult, ALU.add,
                     [pm, self.CT, self.vecsb], [m1])
            m2 = self.ftile()
            self.tt_("dve", m2.ap[:, 0:TT], pu.ap[:, :], m1.ap[:, 0:TT], ALU.mult, [pu, m1], [m2])
            yb = self.Y[cl][tt]
            self.tt_("pool", yb.ap, m2.ap[:, 0:TT], sz.ap[:, 0:TT], ALU.mult, [m2, sz], [yb])
        self.release(iu)
        self.release(iz)

    def odd_p(self, j, g, tt, ip):
        W = 2 << g
        lev = g + 1
        pooled = []
        for cc in range(4):
            pp = self.proj_fm(ip[cc], tt)
            pt = self.ftile()
            if tt == 0:
                self.memset("dve", pt.ap[:, 0:16], 0.0, [pt])
            else:
                self.P.op("act", lambda e, pt=pt, cc=cc: e.activation(out=pt.ap[:, 0:16], in_=self.halo[:, cc, :], func=AF.Copy),
                          [self.halob[cc]], [pt])
            self.act(pt.ap[:, 16:FW], pp.ap[:, :], AF.Copy, [pp], [pt])
            if tt < NT - 1:
                self.act(self.halo[:, cc, :], pp.ap[:, TT - 16:TT], AF.Copy, [pp], [self.halob[cc]])
            cur = pt
            lo = 0
            for lv in range(lev):
                sh = 1 << lv
                lo2 = lo + sh
                nt_ = self.ftile()
                self.tt_("dve", nt_.ap[:, lo2:FW], cur.ap[:, lo2:FW], cur.ap[:, lo2 - sh:FW - sh], ALU.add, [cur], [nt_])
                cur = nt_
                lo = lo2
            pl = self.bqtile()
            self.stt(pl.ap, cur.ap[:, 16:FW], 1.0 / W, pt.ap[:, 16:FW], ALU.mult, ALU.subtract, [cur, pt], [pl])
            if tt == 0:
                n = W - 1
                self.tt_("dve", self.fix[:, 0:n], cur.ap[:, 16:16 + n], self.invc[:, g, 0:n], ALU.mult,
                         [cur, self.constb], [self.fixb])
                self.tt_("dve", pl.ap[:, 0:n], self.fix[:, 0:n], pt.ap[:, 16:16 + n], ALU.subtract,
                         [self.fixb, pt], [pl])
            pooled.append(pl)
        return pooled

    def odd_zy(self, j, g, tt, izs, ipw, pooled):
        szs = []
        for m in range(4):
            pz = self.proj_fm(izs[m], tt)
            sz = self.ftile()
            self.act(sz.ap[:, 0:TT], pz.ap[:, :], AF.Silu, [pz], [sz])
            szs.append(sz)
        for m in range(4):
            sl = self.slot(ipw[m // 2])
            py = self.bank()
            pairs = [(sl.ap[:, ((m % 2) * 4 + kk) * 128:((m % 2) * 4 + kk + 1) * 128], pooled[kk].ap) for kk in range(4)]
            self.mm_group(py, py.ap[:, :], pairs, [sl] + pooled)
            psc = self.vecs[:, V_PS + j * 16 + g * 4 + m:V_PS + j * 16 + g * 4 + m + 1]
            yb = self.Y[m][tt]
            self.stt(yb.ap, py.ap[:, :], psc, szs[m].ap[:, 0:TT], ALU.mult, ALU.mult, [py, szs[m], self.vecsb], [yb])

    def emit_odd_layer(self, l, j, nxt, mst, cbn):
        steps = [(g, tt) for g in range(4) for tt in range(NT)]
        ips, rest, pooled = {}, {}, {}

        def acq_p(g):
            ips[g] = [self.acquire(("win_o", j, g * 512 + cc * 128)) for cc in range(4)]

        def do_p(g, tt):
            pooled[(g, tt)] = self.odd_p(j, g, tt, ips[g])
            if tt == NT - 1:
                for i in ips[g]:
                    self.release(i)

        acq_p(0)
        do_p(0, 0)
        if self._gate0_pending:
            self.emit_mod_part(l, 16, 24, {})
            self._gate0_pending = False
        for si, (g, tt) in enumerate(steps):
            if tt == 0:
                izs = [self.acquire(("win_o", j, 2048 + g * 512 + m * 128)) for m in range(4)]
                ipw = [self.acquire(("poolw", j, g, mp)) for mp in range(2)]
                rest[g] = (izs, ipw)
            if si + 1 < len(steps):
                g2, t2 = steps[si + 1]
                if t2 == 0:
                    acq_p(g2)
                do_p(g2, t2)
            izs, ipw = rest[g]
            self.odd_zy(j, g, tt, izs, ipw, pooled.pop((g, tt)))
            if tt == NT - 2 and nxt is not None:
                self.emit_mod_part(nxt, g * 6, (g + 1) * 6, mst)
            if tt == NT - 1:
                for i in izs + ipw:
                    self.release(i)
                self.emit_outproj(l, "c", j, g * 512, cbn if g == 3 else None)

    def emit(self):
        self.setup_mem()
        self.emit_setup()
        layers = self.layers
        if layers:
            sets0 = [self.norm_a1(tt) for tt in range(NT)]
            self.emit_mod_part(layers[0], 0, 16, {})
            for tt in range(NT):
                self.norm_b(layers[0], tt, self.norm_a2(sets0[tt]))
        self._gate0_pending = bool(layers)
        for li, l in enumerate(layers):
            nxt = layers[li + 1] if li + 1 < len(layers) else None
            mst = {}
            j = l // 2
            if l % 2 == 1 and nxt is not None and nxt % 2 == 0:
                self.emit_sgu_loads(nxt // 2)
            if li == 0 and l % 2 == 0:
                self.emit_sgu_loads(j)
            if nxt is not None:
                cbn = self.norm_pipeline(nxt)
            elif self.do_final:
                cbn = self.norm_pipeline(None)
            else:
                cbn = None
            if l % 2 == 0:
                self.emit_even_A(l, j, 0)
                self.emit_sgu_consts(j)
                if self._gate0_pending:
                    self.emit_mod_part(l, 16, 24, {})
                    self._gate0_pending = False
                if nxt is not None:
                    self.emit_mod_part(nxt, 0, 6, mst)
                self.emit_outproj(l, "ab", j, 0)
                self.emit_even_A(l, j, 4)
                vh = self.emit_sgu_stage1(j, 0)
                for bi in range(NT):
                    vh(bi)
                if nxt is not None:
                    self.emit_mod_part(nxt, 6, 12, mst)
                self.emit_outproj(l, "ab", j, 512)
                for cl in range(4):
                    vh = self.emit_sgu_stage1(j, cl + 1)
                    self.emit_sgu_stage2(j, cl, cl, vh)
                if nxt is not None:
                    self.emit_mod_part(nxt, 12, 18, mst)
                self.emit_outproj(l, "ab", j, 1024)
                for cl in range(4):
                    vh = self.emit_sgu_stage1(j, 4 + cl + 1) if 4 + cl + 1 < 8 else None
                    self.emit_sgu_stage2(j, 4 + cl, cl, vh)
                if nxt is not None:
                    self.emit_mod_part(nxt, 18, 24, mst)
                self.emit_outproj(l, "ab", j, 1536, cbn)
            else:
                self.emit_odd_layer(l, j, nxt, mst, cbn)
        if not layers and self.do_final:
            cbf = self.norm_pipeline(None)
            for tt in range(NT + 1):
                cbf(tt)
        if not self.do_final:
            self.emit_store_x()
        for s in self.outsem:
            if self.P.cnt.get(s, 0) > 0:
                self.P.wait("sp", (s, self.P.cnt[s]))

    def finish(self):
        nc, P = self.nc, self.P
        nblk = max(1, len(self.descs))
        self.wblk = nc.dram_tensor("wblk", [nblk, 128, 1024], F32, kind="ExternalInput").ap()
        for k in P.semkeys:
            P.sems[k] = self.es.enter_context(nc.semaphore(k))

        def replay(eng, e):
            for it in P.q[eng]:
                if it[0] == "wait":
                    e.wait_ge(P.sems[it[1]], it[2])
                elif it[0] == "dma_lazy":
                    ins = it[1](e)
                    if ins is not None:
                        ins.then_inc(P.sems[it[2]], it[3])
                else:
                    ins = it[1](e)
                    if it[2] is not None:
                        ins.then_inc(P.sems[it[2]], it[3])

        with nc.Block() as block:
            @block.sync
            def _(e):
                replay("sp", e)

            @block.scalar
            def _(e):
                replay("act", e)

            @block.vector
            def _(e):
                replay("dve", e)

            @block.gpsimd
            def _(e):
                replay("pool", e)

            @block.tensor
            def _(e):
                replay("pe", e)


def build(layers, do_final):
    nc = bass.Bass("TRN2", target_bir_lowering=False)
    with ExitStack() as es:
        em = Emitter(nc, es, layers, do_final)
        em.emit()
        em.finish()
    return nc, em.descs


def _colmajor(v):
    return np.ascontiguousarray(v.reshape(-1, 128).T)


def _make_block(desc, inp):
    kind = desc[0]
    if kind == "ada":
        _, l, mc = desc
        w = inp["ada_w"][l][:, mc * 128:(mc + 1) * 128]
        return w.reshape(8, 128, 128).transpose(1, 0, 2).reshape(128, 1024)
    if kind in ("win_e", "win_o"):
        _, j, col0 = desc
        src = inp["ab_w_in"] if kind == "win_e" else inp["c_w_in"]
        w = src[j][:, col0:col0 + 128]
        return w.reshape(8, 128, 128).transpose(1, 0, 2).reshape(128, 1024)
    if kind == "wout":
        _, arr, j, row0, mp = desc
        src = inp["ab_w_out"] if arr == "ab" else inp["c_w_out"]
        w = src[j][row0:row0 + 512, mp * 256:(mp + 1) * 256]
        return w.reshape(4, 128, 2, 128).transpose(1, 2, 0, 3).reshape(128, 1024)
    if kind == "poolw":
        _, j, g, mp = desc
        w = inp["c_pool_w"][j, g][:, mp * 256:(mp + 1) * 256]
        return w.reshape(4, 128, 2, 128).transpose(1, 2, 0, 3).reshape(128, 1024)
    raise ValueError(kind)


def _pack_vecs(inp):
    v = np.zeros((128, NV), np.float32)
    for l in range(DEPTH):
        v[:, V_NG + l * 8:V_NG + (l + 1) * 8] = _colmajor(inp["norm_g"][l])
        v[:, V_AB + l * 24:V_AB + (l + 1) * 24] = _colmajor(inp["ada_b"][l])
    v[:, V_FG:V_FG + 8] = _colmajor(inp["final_g"])
    for j in range(2):
        for tap in range(3):
            o = V_CW + (j * 3 + tap) * 8
            v[:, o:o + 8] = _colmajor(inp["ab_conv_w"][j, tap])
        v[:, V_LG + j * 8:V_LG + (j + 1) * 8] = _colmajor(inp["ab_ln_g"][j])
        v[:, V_LB + j * 8:V_LB + (j + 1) * 8] = _colmajor(inp["ab_ln_b"][j])
        v[:, V_PS + j * 16:V_PS + (j + 1) * 16] = _colmajor(inp["c_pool_scale"][j])
    return v


_CACHE = {}


def _get_prog(layers, do_final):
    key = (tuple(layers), do_final)
    if key not in _CACHE:
        _CACHE[key] = build(list(layers), do_final)
    return _CACHE[key]


def _run(layers, do_final, xT_list, inp):
    nc, descs = _get_prog(layers, do_final)
    if descs:
        wblk = np.empty((len(descs), 128, 1024), np.float32)
        for i, d in enumerate(descs):
            wblk[i] = _make_block(d, inp)
    else:
        wblk = np.zeros((1, 128, 1024), np.float32)
    vecs = _pack_vecs(inp)
    sgub = np.ascontiguousarray(inp["ab_sgu_b"].reshape(2, 1024))
    sguT = np.ascontiguousarray(inp["ab_sgu_w"].transpose(0, 3, 1, 2).reshape(2, 128, 1024))
    in_maps = []
    for b in range(8):
        in_maps.append({
            "xT": xT_list[b],
            "cvec": _colmajor(inp["c"][b]),
            "vecs": vecs,
            "sgub": sgub,
            "sguT": sguT,
            "wblk": wblk,
        })
    res = run_bass_kernel_spmd(nc, in_maps, core_ids=list(range(8)))
    return [np.asarray(r["outT"]) for r in res.results]


def kernel(**inputs):
    inp = {k: np.asarray(v, dtype=np.float32) for k, v in inputs.items()}
    x = inp["x"]
    xT = [np.ascontiguousarray(x[b].T) for b in range(8)]
    if FUSED:
        outT = _run([0, 1, 2, 3], True, xT, inp)
    else:
        for l in range(DEPTH):
            xT = _run([l], l == DEPTH - 1, xT, inp)
        outT = xT
    out = np.stack([np.ascontiguousarray(o.T) for o in outT], axis=0)
    return out.astype(np.float32)
```

```python
import numpy as np
from contextlib import ExitStack
import concourse.bass as bass
import concourse.mybir as mybir
from concourse.bass_utils import run_bass_kernel_spmd

F32 = mybir.dt.float32
BF16 = mybir.dt.bfloat16
AF = mybir.ActivationFunctionType
ALU = mybir.AluOpType
AX = mybir.AxisListType

D = 1024
S = 2048
TT = 512
NT = S // TT
KC = D // 128
DEPTH = 4
EPS = 1e-6
NS = 16
NF = 12
NBQ = 12
FW = 528
NOUTSEM = 16

V_NG = 0
V_FG = 32
V_AB = 40
V_CW = 136
V_LG = 184
V_LB = 200
V_PS = 216
NV = 248

FUSED = True


class Buf:
    __slots__ = ("ap", "w_ev", "r_evs", "excl", "sem")

    def __init__(self, ap=None, excl=False, sem=None):
        self.ap = ap
        self.w_ev = None
        self.r_evs = {}
        self.excl = excl
        self.sem = sem


class Prog:
    ENG = ("pe", "act", "dve", "pool", "sp")

    def __init__(self):
        self.q = {k: [] for k in self.ENG}
        self.cnt = {}
        self.seen = {k: {} for k in self.ENG}
        self.sems = {}
        self.semkeys = list(self.ENG)

    def new_sem(self, key):
        assert key not in self.semkeys
        self.semkeys.append(key)
        return key

    def wait(self, eng, ev):
        if ev is None:
            return
        key, v = ev
        if eng == "pe" and key == "pe":
            return
        if v > self.seen[eng].get(key, 0):
            self.seen[eng][key] = v
            self.q[eng].append(("wait", key, v))

    def _deps(self, eng, reads, writes):
        for b in reads:
            self.wait(eng, b.w_ev)
            if b.excl:
                for k, v in b.r_evs.items():
                    if k != eng:
                        self.wait(eng, (k, v))
        for b in writes:
            self.wait(eng, b.w_ev)
            for k, v in b.r_evs.items():
                if k != eng:
                    self.wait(eng, (k, v))

    def _commit(self, ev, reads, writes):
        for b in reads:
            if ev[1] > b.r_evs.get(ev[0], 0):
                b.r_evs[ev[0]] = ev[1]
        for b in writes:
            b.w_ev = ev
            b.r_evs = {}

    def op(self, eng, fn, reads=(), writes=()):
        self._deps(eng, reads, writes)
        self.cnt[eng] = self.cnt.get(eng, 0) + 1
        ev = (eng, self.cnt[eng])
        self.q[eng].append(("op", fn, eng, 1))
        self._commit(ev, reads, writes)
        return ev

    def pe_group(self, fns, reads=(), writes=()):
        self._deps("pe", reads, writes)
        for fn in fns[:-1]:
            self.q["pe"].append(("op", fn, None, 0))
        self.cnt["pe"] = self.cnt.get("pe", 0) + 1
        ev = ("pe", self.cnt["pe"])
        self.q["pe"].append(("op", fns[-1], "pe", 1))
        self._commit(ev, reads, writes)
        return ev

    def dma(self, eng, fn, sem, reads=(), writes=()):
        self._deps(eng, reads, writes)
        self.cnt[sem] = self.cnt.get(sem, 0) + 16
        ev = (sem, self.cnt[sem])
        self.q[eng].append(("op", fn, sem, 16))
        self._commit(ev, reads, writes)
        return ev

    def replay(self, eng, e):
        for it in self.q[eng]:
            if it[0] == "wait":
                e.wait_ge(self.sems[it[1]], it[2])
            else:
                ins = it[1](e)
                if it[2] is not None:
                    assert ins is not None
                    ins.then_inc(self.sems[it[2]], it[3])


class Emitter:
    def __init__(self, nc, es, layers, do_final):
        self.nc = nc
        self.es = es
        self.layers = layers
        self.do_final = do_final
        self.P = Prog()
        self.descs = []
        self.loaded = set()
        self.n_acq = 0
        self._bank_i = 0
        self._f_i = 0
        self._bq_i = 0
        self._out_i = 0

    def sb(self, name, shape, dt):
        return self.es.enter_context(self.nc.sbuf_tensor(name, shape, dt))

    def setup_mem(self):
        nc, P = self.nc, self.P
        self.xT = nc.dram_tensor("xT", [D, S], F32, kind="ExternalInput").ap()
        self.cvec = nc.dram_tensor("cvec", [128, KC], F32, kind="ExternalInput").ap()
        self.vecs_d = nc.dram_tensor("vecs", [128, NV], F32, kind="ExternalInput").ap()
        self.sgub_d = nc.dram_tensor("sgub", [2, 1024], F32, kind="ExternalInput").ap()
        self.sguT_d = nc.dram_tensor("sguT", [2, 128, 1024], F32, kind="ExternalInput").ap()
        self.outT = nc.dram_tensor("outT", [D, S], F32, kind="ExternalOutput").ap()

        self.Xt = self.sb("X", [128, KC, S], F32)
        self.Ht = self.sb("H", [128, KC, S], BF16)
        self.Yt = self.sb("Y", [128, 4, S], BF16)
        self.ringt = self.sb("ring", [128, NS, 1024], BF16)
        self.Ft = self.sb("F", [128, NF, FW], F32)
        self.BQt = self.sb("BQ", [128, NBQ, TT], BF16)
        self.VHt = self.sb("VH", [128, 2, 16, 128], BF16)
        self.CTt = self.sb("CT", [128, 8, 128], F32)
        self.WMTt = self.sb("WMT", [128, 8, 128], BF16)
        self.vecs = self.sb("vecs_s", [128, NV], F32)
        self.cin = self.sb("cin", [128, KC], F32)
        self.cact = self.sb("cact", [128, KC], BF16)
        self.modv = self.sb("modv", [128, DEPTH * 24], F32)
        self.gsv = self.sb("gsv", [128, DEPTH * 8], F32)
        self.ones = self.sb("ones", [128, 128], BF16)
        self.mhalf = self.sb("mhalf", [128, 1], F32)
        self.onesf = self.sb("onesf", [128, 128], F32)
        self.ident = self.sb("ident", [128, 128], F32)
        self.nrm = self.sb("nrm", [128, 4, 2, 4], F32)
        self.diagt = self.sb("diag", [128, 2, 4, 128], F32)
        self.invc = self.sb("invc", [128, 4, 16], F32)
        self.halo = self.sb("halo", [128, 4, 16], F32)
        self.stat = self.sb("stat", [128, 2, 7, 16], F32)
        self.fix = self.sb("fix", [128, 16], F32)
        self.chh = self.sb("chh", [128, 2], F32)

        self.banks = []
        for i in range(8):
            t = self.es.enter_context(nc.psum_tensor(f"ps{i}", [128, TT], F32))
            self.banks.append(Buf(ap=t, excl=True))

        self.X = [[Buf(ap=self.Xt[:, k, tt * TT:(tt + 1) * TT]) for tt in range(NT)] for k in range(KC)]
        self.H = [[Buf(ap=self.Ht[:, k, tt * TT:(tt + 1) * TT]) for tt in range(NT)] for k in range(KC)]
        self.Y = [[Buf(ap=self.Yt[:, k, tt * TT:(tt + 1) * TT]) for tt in range(NT)] for k in range(4)]
        self.slots = [Buf(ap=self.ringt[:, i, :], sem=P.new_sem(f"rg{i}")) for i in range(NS)]
        self.F = [Buf(ap=self.Ft[:, i, :]) for i in range(NF)]
        self.BQ = [Buf(ap=self.BQt[:, i, :]) for i in range(NBQ)]
        self.VH = [Buf(), Buf()]
        self.CT = Buf()
        self.WMT = Buf()
        self.vecsb = Buf()
        self.cactb = Buf()
        self.cinb = Buf()
        self.modb = [Buf() for _ in range(DEPTH)]
        self.constb = Buf()
        self.halob = [Buf() for _ in range(4)]
        self.statb = [[Buf() for _ in range(7)] for _ in range(2)]
        self.fixb = Buf()
        self.chhb = Buf()
        self.hooks = {}
        self.nrmb = [(Buf(), Buf()) for _ in range(4)]
        self._nrm_i = 0
        self.diagb = [[Buf(ap=self.diagt[:, p, jj, :]) for jj in range(4)] for p in range(2)]
        self.xsem = [P.new_sem(f"xs{k}") for k in range(KC)]
        self.outsem = [P.new_sem(f"os{k}") for k in range(NOUTSEM)]
        self.msem = [P.new_sem(f"ms{k}") for k in range(4)]

    def bank(self):
        b = self.banks[self._bank_i % 8]
        self._bank_i += 1
        return b

    def ftile(self):
        b = self.F[self._f_i % NF]
        self._f_i += 1
        return b

    def bqtile(self):
        b = self.BQ[self._bq_i % NBQ]
        self._bq_i += 1
        return b

    def _ring_load(self, idx):
        slot = self.slots[idx % NS]
        self.loaded.add(idx)

        def fn(e, idx=idx, slot=slot):
            if idx >= len(self.descs):
                return None
            return e.dma_start(out=slot.ap, in_=self.wblk[idx])
        self.P._deps("pool", (), (slot,))
        P = self.P
        P.cnt[slot.sem] = P.cnt.get(slot.sem, 0) + 16
        ev = (slot.sem, P.cnt[slot.sem])
        P.q["pool"].append(("dma_lazy", fn, slot.sem, 16))
        P._commit(ev, (), (slot,))

    def ring_start(self):
        for i in range(NS):
            self._ring_load(i)

    def acquire(self, desc):
        idx = self.n_acq
        self.n_acq += 1
        assert idx in self.loaded, f"ring too small at block {idx} {desc}"
        self.descs.append(desc)
        return idx

    def release(self, idx):
        self._ring_load(idx + NS)

    def slot(self, idx):
        return self.slots[idx % NS]

    def act(self, out, in_, func, reads, writes, scale=None, bias=None):
        kw = {}
        if scale is not None:
            kw["scale"] = scale
        if bias is not None:
            kw["bias"] = bias
        return self.P.op("act", lambda e: e.activation(out=out, in_=in_, func=func, **kw), reads, writes)

    def tt_(self, eng, out, in0, in1, op, reads, writes):
        return self.P.op(eng, lambda e: e.tensor_tensor(out=out, in0=in0, in1=in1, op=op), reads, writes)

    def ts_(self, out, in0, s1, s2, op0, op1, reads, writes):
        if s2 is None:
            return self.P.op("dve", lambda e: e.tensor_scalar(out=out, in0=in0, scalar1=s1, scalar2=None, op0=op0), reads, writes)
        return self.P.op("dve", lambda e: e.tensor_scalar(out=out, in0=in0, scalar1=s1, scalar2=s2, op0=op0, op1=op1), reads, writes)

    def stt(self, out, in0, scalar, in1, op0, op1, reads, writes):
        return self.P.op("dve", lambda e: e.scalar_tensor_tensor(out=out, in0=in0, scalar=scalar, in1=in1, op0=op0, op1=op1), reads, writes)

    def memset(self, eng, ap, val, writes):
        return self.P.op(eng, lambda e: e.memset(ap, val), (), writes)

    def mm_group(self, bankbuf, out_ap, pairs, reads):
        n = len(pairs)
        fns = []
        for i, (l, r) in enumerate(pairs):
            fns.append(lambda e, l=l, r=r, i=i: e.matmul(out_ap, l, r, start=(i == 0), stop=(i == n - 1)))
        return self.P.pe_group(fns, reads, [bankbuf])

    def emit_setup(self):
        P = self.P
        self.memset("dve", self.ones[:], 1.0, [self.constb])
        self.memset("dve", self.mhalf[:], -0.5, [self.constb])
        self.memset("dve", self.onesf[:], 1.0, [self.constb])
        self.memset("pool", self.ident[:], 1.0, [self.constb])
        self.P.op("pool", lambda e: e.affine_select(out=self.ident[:], in_=self.ident[:], pattern=[[1, 128]],
                                                     compare_op=ALU.is_equal, fill=0.0, base=0, channel_multiplier=-1),
                  [self.constb], [self.constb])
        for g in range(4):
            W = 2 << g
            self.memset("dve", self.invc[:, g, :], 1.0 / W, [self.constb])
            for t in range(W - 1):
                self.memset("dve", self.invc[:, g, t:t + 1], 1.0 / (t + 1), [self.constb])
        P.dma("sp", lambda e: e.dma_start(out=self.vecs[:], in_=self.vecs_d), self.msem[0], (), [self.vecsb])
        P.dma("sp", lambda e: e.dma_start(out=self.cin[:], in_=self.cvec), self.msem[1], (), [self.cinb])
        for k in range(KC):
            P.dma("sp", lambda e, k=k: e.dma_start(out=self.Xt[:, k, :], in_=self.xT[k * 128:(k + 1) * 128, :]),
                  self.xsem[k], (), self.X[k])
        self.act(self.cact[:], self.cin[:], AF.Silu, [self.cinb], [self.cactb])
        self.ring_start()

    def emit_mod_part(self, l, mc0, mc1, st):
        bk = self.bank()
        mb = self.modb[l]
        for mc in range(mc0, mc1):
            idx = self.acquire(("ada", l, mc))
            sl = self.slot(idx)
            pairs = [(sl.ap[:, k * 128:(k + 1) * 128], self.cact[:, k:k + 1]) for k in range(KC)]
            self.mm_group(bk, bk.ap[:, mc:mc + 1], pairs, [sl, self.cactb])
            self.release(idx)
        self.tt_("dve", self.modv[:, l * 24 + mc0:l * 24 + mc1], bk.ap[:, mc0:mc1],
                 self.vecs[:, V_AB + l * 24 + mc0:V_AB + l * 24 + mc1], ALU.add, [bk, self.vecsb], [mb])
        if mc0 < 16 <= mc1:
            self.stt(self.gsv[:, l * 8:(l + 1) * 8], self.modv[:, l * 24 + 8:l * 24 + 16], 1.0,
                     self.vecs[:, V_NG + l * 8:V_NG + (l + 1) * 8], ALU.add, ALU.mult, [mb, self.vecsb], [mb])

    def shift(self, l, k):
        return self.modv[:, l * 24 + k:l * 24 + k + 1]

    def gate(self, l, k):
        return self.modv[:, l * 24 + 16 + k:l * 24 + 16 + k + 1]

    def gs(self, l, k):
        return self.gsv[:, l * 8 + k:l * 8 + k + 1]

    def norm_sq(self, tt):
        sq = []
        for k in range(KC):
            q = self.bqtile()
            self.act(q.ap, self.X[k][tt].ap, AF.Square, [self.X[k][tt]], [q])
            sq.append(q)
        return sq

    def norm_a1pe(self, sq):
        bk = self.bank()
        fns = []
        for jj in range(4):
            for k in range(KC):
                lhsT = sq[k].ap[:, jj * 128:(jj + 1) * 128]
                out = bk.ap[:, jj:jj + 1]
                fns.append(lambda e, out=out, lhsT=lhsT, k=k: e.matmul(out, lhsT, self.ones[:, 0:1], start=(k == 0), stop=(k == KC - 1)))
        self.P.pe_group(fns, sq + [self.constb], [bk])
        ns = self._nrm_i % 4
        self._nrm_i += 1
        tb, rb = self.nrmb[ns]
        tap = self.nrm[:, ns, 0, :]
        rap = self.nrm[:, ns, 1, :]
        self.ts_(tap, bk.ap[:, 0:4], 1.0 / D, EPS, ALU.mult, ALU.add, [bk], [tb])
        self.tt_("pool", rap, tap, self.mhalf[:, 0:1].to_broadcast([128, 4]), ALU.pow, [tb, self.constb], [rb])
        return ns

    def norm_a1(self, tt):
        return self.norm_a1pe(self.norm_sq(tt))

    def norm_a2(self, ns):
        par = ns % 2
        rb = self.nrmb[ns][1]
        diags = []
        for jj in range(4):
            dg = self.diagb[par][jj]
            self.ts_(dg.ap, self.ident[:], self.nrm[:, ns, 1, jj:jj + 1], None, ALU.mult, None,
                     [rb, self.constb], [dg])
            diags.append(dg)
        return diags

    def norm_a(self, tt):
        return self.norm_a2(self.norm_a1(tt))

    def norm_r(self, diags):
        bk = self.bank()
        fns = []
        for jj in range(4):
            out = bk.ap[:, jj * 128:(jj + 1) * 128]
            rhs = diags[jj].ap
            fns.append(lambda e, out=out, rhs=rhs: e.matmul(out, self.onesf[:], rhs, start=True, stop=True))
        self.P.pe_group(fns, diags + [self.constb], [bk])
        return bk

    def norm_b(self, l, tt, diags, split=True):
        rk = self.norm_r(diags)
        mb = self.modb[l]
        npool = 3 if split else 0
        rs = None
        if npool:
            rs = self.ftile()
            self.act(rs.ap[:, 0:TT], rk.ap[:, :], AF.Copy, [rk], [rs])
        for k in range(KC):
            xn = self.ftile()
            if k < KC - npool:
                self.tt_("dve", xn.ap[:, 0:TT], rk.ap[:, :], self.X[k][tt].ap, ALU.mult, [self.X[k][tt], rk], [xn])
                self.act(self.H[k][tt].ap, xn.ap[:, 0:TT], AF.Identity, [xn, mb], [self.H[k][tt]],
                         scale=self.gs(l, k), bias=self.shift(l, k))
            else:
                self.tt_("pool", xn.ap[:, 0:TT], rs.ap[:, 0:TT], self.X[k][tt].ap, ALU.mult, [self.X[k][tt], rs], [xn])
                self.P.op("pool", lambda e, xn=xn, k=k: e.tensor_scalar(
                    out=self.H[k][tt].ap, in0=xn.ap[:, 0:TT], scalar1=self.gs(l, k), scalar2=self.shift(l, k),
                    op0=ALU.mult, op1=ALU.add), [xn, mb], [self.H[k][tt]])

    def final_b(self, tt, diags):
        rk = self.norm_r(diags)
        for k in range(KC):
            o = self.ftile()
            self.stt(o.ap[:, 0:TT], self.X[k][tt].ap, self.vecs[:, V_FG + k:V_FG + k + 1], rk.ap[:, :],
                     ALU.mult, ALU.mult, [self.X[k][tt], rk, self.vecsb], [o])
            sem = self.outsem[self._out_i % NOUTSEM]
            self._out_i += 1
            self.P.dma("sp", lambda e, o=o, k=k, tt=tt: e.dma_start(
                out=self.outT[k * 128:(k + 1) * 128, tt * TT:(tt + 1) * TT], in_=o.ap[:, 0:TT]), sem, [o], ())

    def norm_pipeline(self, l):
        st = {}

        def bfn(t):
            if l is None:
                self.final_b(t, st[t])
            else:
                self.norm_b(l, t, st[t])

        sqs = {}

        def cb(tt):
            if tt >= 1:
                st[tt - 1] = self.norm_a2(self.norm_a1pe(sqs.pop(tt - 1)))
            if tt < NT:
                sqs[tt] = self.norm_sq(tt)
            if tt >= 2:
                bfn(tt - 2)
            if tt == NT:
                if l is None:
                    bfn(NT - 1)
                else:
                    self.hooks[1] = lambda: bfn(NT - 1)
        return cb

    def emit_store_x(self):
        for k in range(KC):
            sem = self.outsem[self._out_i % NOUTSEM]
            self._out_i += 1
            self.P.dma("sp", lambda e, k=k: e.dma_start(out=self.outT[k * 128:(k + 1) * 128, :], in_=self.Xt[:, k, :]),
                       sem, self.X[k], ())

    def use_tile(self, tt):
        h = self.hooks.pop(tt, None)
        if h is not None:
            h()

    def proj_fm(self, idx, tt):
        self.use_tile(tt)
        sl = self.slot(idx)
        bk = self.bank()
        pairs = [(sl.ap[:, k * 128:(k + 1) * 128], self.H[k][tt].ap) for k in range(KC)]
        self.mm_group(bk, bk.ap[:, :], pairs, [sl] + [self.H[k][tt] for k in range(KC)])
        return bk

    def emit_outproj(self, l, arr, j, row0, cb=None):
        idxs = [self.acquire(("wout", arr, j, row0, mp)) for mp in range(4)]
        mb = self.modb[l]
        for tt in range(NT):
            for m in range(KC):
                sl = self.slot(idxs[m // 2])
                bk = self.bank()
                pairs = [(sl.ap[:, ((m % 2) * 4 + kk) * 128:((m % 2) * 4 + kk + 1) * 128], self.Y[kk][tt].ap)
                         for kk in range(4)]
                self.mm_group(bk, bk.ap[:, :], pairs, [sl] + [self.Y[kk][tt] for kk in range(4)])
                xb = self.X[m][tt]
                self.stt(xb.ap, bk.ap[:, :], self.gate(l, m), xb.ap, ALU.mult, ALU.add, [bk, xb, mb], [xb])
            if cb is not None:
                cb(tt)
        for i in idxs:
            self.release(i)
        if cb is not None:
            cb(NT)

    def emit_even_A(self, l, j, c0):
        for cl in range(4):
            c = c0 + cl
            iz = self.acquire(("win_e", j, 3 * 1024 + c * 128))
            ih = self.acquire(("win_e", j, 0 * 1024 + c * 128))
            iC = self.acquire(("win_e", j, 2 * 1024 + c * 128))
            iB = self.acquire(("win_e", j, 1 * 1024 + c * 128))
            w0 = self.vecs[:, V_CW + (j * 3 + 0) * 8 + c:V_CW + (j * 3 + 0) * 8 + c + 1]
            w1 = self.vecs[:, V_CW + (j * 3 + 1) * 8 + c:V_CW + (j * 3 + 1) * 8 + c + 1]
            w2 = self.vecs[:, V_CW + (j * 3 + 2) * 8 + c:V_CW + (j * 3 + 2) * 8 + c + 1]
            prev = None
            for tt in range(NT):
                pz = self.proj_fm(iz, tt)
                ph = self.proj_fm(ih, tt)
                pC = self.proj_fm(iC, tt)
                pB = self.proj_fm(iB, tt)
                sz = self.ftile()
                self.act(sz.ap[:, 0:TT], pz.ap[:, :], AF.Silu, [pz], [sz])
                ah = self.ftile()
                self.act(ah.ap[:, 0:TT], ph.ap[:, :], AF.Copy, [ph], [ah])
                ch = self.ftile()
                self.tt_("dve", ch.ap[:, 16:FW], pC.ap[:, :], ah.ap[:, 0:TT], ALU.mult, [pC, ah], [ch])
                if tt == 0:
                    self.memset("dve", ch.ap[:, 14:16], 0.0, [ch])
                else:
                    self.P.op("dve", lambda e, ch=ch: e.tensor_copy(out=ch.ap[:, 14:16], in_=self.chh[:, 0:2]),
                              [self.chhb], [ch])
                if tt < NT - 1:
                    self.P.op("dve", lambda e, ch=ch: e.tensor_copy(out=self.chh[:, 0:2], in_=ch.ap[:, FW - 2:FW]),
                              [ch], [self.chhb])
                t0 = self.ftile()
                self.act(t0.ap[:, 0:TT], ch.ap[:, 16:FW], AF.Copy, [ch, self.vecsb], [t0], scale=w2)
                t1 = self.ftile()
                self.stt(t1.ap[:, 0:TT], ch.ap[:, 15:FW - 1], w1, t0.ap[:, 0:TT], ALU.mult, ALU.add, [ch, t0, self.vecsb], [t1])
                t2 = self.ftile()
                self.stt(t2.ap[:, 0:TT], ch.ap[:, 14:FW - 2], w0, t1.ap[:, 0:TT], ALU.mult, ALU.add, [ch, t1, self.vecsb], [t2])
                u = self.ftile()
                self.tt_("dve", u.ap[:, 0:TT], pB.ap[:, :], t2.ap[:, 0:TT], ALU.mult, [pB, t2], [u])
                yb = self.Y[cl][tt]
                self.tt_("dve", yb.ap, u.ap[:, 0:TT], sz.ap[:, 0:TT], ALU.mult, [u, sz], [yb])
                prev = ch
            for i in (iz, ih, iC, iB):
                self.release(i)

    def emit_sgu_loads(self, j):
        P = self.P
        P.dma("pool", lambda e: e.dma_start(out=self.WMTt[:].rearrange("p a b -> p (a b)"), in_=self.sguT_d[j]),
              self.msem[2], (), [self.WMT])
        P.op("pool", lambda e: e.affine_select(out=self.WMTt[:], in_=self.WMTt[:], pattern=[[0, 8], [1, 128]],
                                                compare_op=ALU.is_ge, fill=0.0, base=0, channel_multiplier=-1),
             [self.WMT], [self.WMT])
        P.dma("sp", lambda e: e.dma_start(out=self.CTt[:].rearrange("p a b -> p (a b)"),
                                          in_=self.sgub_d[j:j + 1, :].to_broadcast([128, 1024])),
              self.msem[3], (), [self.CT])

    def emit_sgu_consts(self, j):
        for half in range(2):
            bk = self.bank()
            rhs = self.WMTt[:, half * 4:(half + 1) * 4, :].rearrange("p a b -> p (a b)")
            self.mm_group(bk, bk.ap[:, :], [(self.ones[:], rhs)], [self.WMT, self.constb])
            for hh in range(4):
                c = half * 4 + hh
                lnb = self.vecs[:, V_LB + j * 8 + c:V_LB + j * 8 + c + 1]
                self.stt(self.CTt[:, c, :], bk.ap[:, hh * 128:(hh + 1) * 128], lnb, self.CTt[:, c, :],
                         ALU.mult, ALU.add, [bk, self.CT, self.vecsb], [self.CT])

    def emit_sgu_stage1(self, j, c):
        par = c % 2
        iv = self.acquire(("win_e", j, 5 * 1024 + c * 128))
        sl = self.slot(iv)
        st = self.stat
        sb_ = self.statb[par]
        vts = []
        for bi in range(NT):
            bk = self.bank()
            fns = []
            for jj in range(4):
                for k in range(KC):
                    lhsT = self.Ht[:, k, bi * TT + jj * 128:bi * TT + (jj + 1) * 128]
                    rhs = sl.ap[:, k * 128:(k + 1) * 128]
                    out = bk.ap[:, jj * 128:(jj + 1) * 128]
                    fns.append(lambda e, out=out, lhsT=lhsT, rhs=rhs, k=k: e.matmul(out, lhsT, rhs, start=(k == 0), stop=(k == KC - 1)))
            self.P.pe_group(fns, [sl] + [self.H[k][bi] for k in range(KC)], [bk])
            vs = self.ftile()
            self.act(vs.ap[:, 0:TT], bk.ap[:, :], AF.Copy, [bk], [vs])
            sq = self.ftile()
            self.act(sq.ap[:, 0:TT], bk.ap[:, :], AF.Square, [bk], [sq])
            self.P.op("dve", lambda e, vs=vs, bi=bi: e.tensor_reduce(
                out=st[:, par, 0, bi * 4:(bi + 1) * 4], in_=vs.ap[:, 0:TT].rearrange("p (a b) -> p a b", a=4),
                axis=AX.X, op=ALU.add), [vs], [sb_[0]])
            self.P.op("dve", lambda e, sq=sq, bi=bi: e.tensor_reduce(
                out=st[:, par, 1, bi * 4:(bi + 1) * 4], in_=sq.ap[:, 0:TT].rearrange("p (a b) -> p a b", a=4),
                axis=AX.X, op=ALU.add), [sq], [sb_[1]])
            vts.append(vs)
        self.release(iv)
        SUM, SSQ, MEAN, MSQ, T_, RSTD, NMR = [st[:, par, i, :] for i in range(7)]
        self.ts_(MEAN, SUM, 1.0 / 128, None, ALU.mult, None, [sb_[0]], [sb_[2]])
        self.tt_("dve", MSQ, MEAN, MEAN, ALU.mult, [sb_[2]], [sb_[3]])
        self.stt(T_, SSQ, 1.0 / 128, MSQ, ALU.mult, ALU.subtract, [sb_[1], sb_[3]], [sb_[4]])
        self.ts_(T_, T_, EPS, None, ALU.add, None, [sb_[4]], [sb_[4]])
        self.tt_("pool", RSTD, T_, self.mhalf[:, 0:1].to_broadcast([128, 16]), ALU.pow, [sb_[4], self.constb], [sb_[5]])
        self.stt(NMR, MEAN, -1.0, RSTD, ALU.mult, ALU.mult, [sb_[2], sb_[5]], [sb_[6]])

        def vhat(bi):
            for jj in range(4):
                jx = bi * 4 + jj
                self.act(self.VHt[:, par, jx, :], vts[bi].ap[:, jj * 128:(jj + 1) * 128], AF.Identity,
                         [vts[bi], sb_[5], sb_[6]], [self.VH[par]],
                         scale=st[:, par, 5, jx:jx + 1], bias=st[:, par, 6, jx:jx + 1])
        return vhat

    def emit_sgu_stage2(self, j, c, cl, next_vhat=None):
        par = c % 2
        iu = self.acquire(("win_e", j, 4 * 1024 + c * 128))
        iz = self.acquire(("win_e", j, 6 * 1024 + c * 128))
        lng = self.vecs[:, V_LG + j * 8 + c:V_LG + j * 8 + c + 1]
        for tt in range(NT):
            if next_vhat is not None:
                next_vhat(tt)
            pm = self.bank()
            fns = []
            for jj in range(4):
                lhsT = self.VHt[:, par, tt * 4 + jj, :]
                rhs = self.WMTt[:, c, :]
                out = pm.ap[:, jj * 128:(jj + 1) * 128]
                fns.append(lambda e, out=out, lhsT=lhsT, rhs=rhs: e.matmul(out, lhsT, rhs, start=True, stop=True))
            self.P.pe_group(fns, [self.VH[par], self.WMT], [pm])
            pu = self.proj_fm(iu, tt)
            pz = self.proj_fm(iz, tt)
            sz = self.ftile()
            self.act(sz.ap[:, 0:TT], pz.ap[:, :], AF.Silu, [pz], [sz])
            m1 = self.ftile()
            self.stt(m1.ap[:, 0:TT].rearrange("p (a b) -> p a b", a=4), pm.ap[:, :].rearrange("p (a b) -> p a b", a=4),
                     lng, self.CTt[:, c, :].unsqueeze(1).to_broadcast([128, 4, 128]), ALU.mult, ALU.add,
                     [pm, self.CT, self.vecsb], [m1])
            m2 = self.ftile()
            self.tt_("dve", m2.ap[:, 0:TT], pu.ap[:, :], m1.ap[:, 0:TT], ALU.mult, [pu, m1], [m2])
            yb = self.Y[cl][tt]
            self.tt_("pool", yb.ap, m2.ap[:, 0:TT], sz.ap[:, 0:TT], ALU.mult, [m2, sz], [yb])
        self.release(iu)
        self.release(iz)

    def odd_p(self, j, g, tt, ip):
        W = 2 << g
        lev = g + 1
        pooled = []
        for cc in range(4):
            pp = self.proj_fm(ip[cc], tt)
            pt = self.ftile()
            if tt == 0:
                self.memset("dve", pt.ap[:, 0:16], 0.0, [pt])
            else:
                self.P.op("act", lambda e, pt=pt, cc=cc: e.activation(out=pt.ap[:, 0:16], in_=self.halo[:, cc, :], func=AF.Copy),
                          [self.halob[cc]], [pt])
            self.act(pt.ap[:, 16:FW], pp.ap[:, :], AF.Copy, [pp], [pt])
            if tt < NT - 1:
                self.act(self.halo[:, cc, :], pp.ap[:, TT - 16:TT], AF.Copy, [pp], [self.halob[cc]])
            cur = pt
            lo = 0
            for lv in range(lev):
                sh = 1 << lv
                lo2 = lo + sh
                nt_ = self.ftile()
                self.tt_("dve", nt_.ap[:, lo2:FW], cur.ap[:, lo2:FW], cur.ap[:, lo2 - sh:FW - sh], ALU.add, [cur], [nt_])
                cur = nt_
                lo = lo2
            pl = self.bqtile()
            self.stt(pl.ap, cur.ap[:, 16:FW], 1.0 / W, pt.ap[:, 16:FW], ALU.mult, ALU.subtract, [cur, pt], [pl])
            if tt == 0:
                n = W - 1
                self.tt_("dve", self.fix[:, 0:n], cur.ap[:, 16:16 + n], self.invc[:, g, 0:n], ALU.mult,
                         [cur, self.constb], [self.fixb])
                self.tt_("dve", pl.ap[:, 0:n], self.fix[:, 0:n], pt.ap[:, 16:16 + n], ALU.subtract,
                         [self.fixb, pt], [pl])
            pooled.append(pl)
        return pooled

    def odd_zy(self, j, g, tt, izs, ipw, pooled):
        szs = []
        for m in range(4):
            pz = self.proj_fm(izs[m], tt)
            sz = self.ftile()
            self.act(sz.ap[:, 0:TT], pz.ap[:, :], AF.Silu, [pz], [sz])
            szs.append(sz)
        for m in range(4):
            sl = self.slot(ipw[m // 2])
            py = self.bank()
            pairs = [(sl.ap[:, ((m % 2) * 4 + kk) * 128:((m % 2) * 4 + kk + 1) * 128], pooled[kk].ap) for kk in range(4)]
            self.mm_group(py, py.ap[:, :], pairs, [sl] + pooled)
            psc = self.vecs[:, V_PS + j * 16 + g * 4 + m:V_PS + j * 16 + g * 4 + m + 1]
            yb = self.Y[m][tt]
            self.stt(yb.ap, py.ap[:, :], psc, szs[m].ap[:, 0:TT], ALU.mult, ALU.mult, [py, szs[m], self.vecsb], [yb])

    def emit_odd_layer(self, l, j, nxt, mst, cbn):
        steps = [(g, tt) for g in range(4) for tt in range(NT)]
        ips, rest, pooled = {}, {}, {}

        def acq_p(g):
            ips[g] = [self.acquire(("win_o", j, g * 512 + cc * 128)) for cc in range(4)]

        def do_p(g, tt):
            pooled[(g, tt)] = self.odd_p(j, g, tt, ips[g])
            if tt == NT - 1:
                for i in ips[g]:
                    self.release(i)

        acq_p(0)
        do_p(0, 0)
        if self._gate0_pending:
            self.emit_mod_part(l, 16, 24, {})
            self._gate0_pending = False
        for si, (g, tt) in enumerate(steps):
            if tt == 0:
                izs = [self.acquire(("win_o", j, 2048 + g * 512 + m * 128)) for m in range(4)]
                ipw = [self.acquire(("poolw", j, g, mp)) for mp in range(2)]
                rest[g] = (izs, ipw)
            if si + 1 < len(steps):
                g2, t2 = steps[si + 1]
                if t2 == 0:
                    acq_p(g2)
                do_p(g2, t2)
            izs, ipw = rest[g]
            self.odd_zy(j, g, tt, izs, ipw, pooled.pop((g, tt)))
            if tt == NT - 2 and nxt is not None:
                self.emit_mod_part(nxt, g * 6, (g + 1) * 6, mst)
            if tt == NT - 1:
                for i in izs + ipw:
                    self.release(i)
                self.emit_outproj(l, "c", j, g * 512, cbn if g == 3 else None)

    def emit(self):
        self.setup_mem()
        self.emit_setup()
        layers = self.layers
        if layers:
            sets0 = [self.norm_a1(tt) for tt in range(NT)]
            self.emit_mod_part(layers[0], 0, 16, {})
            for tt in range(NT):
                self.norm_b(layers[0], tt, self.norm_a2(sets0[tt]), split=False)
        self._gate0_pending = bool(layers)
        for li, l in enumerate(layers):
            nxt = layers[li + 1] if li + 1 < len(layers) else None
            mst = {}
            j = l // 2
            if l % 2 == 1 and nxt is not None and nxt % 2 == 0:
                self.emit_sgu_loads(nxt // 2)
            if li == 0 and l % 2 == 0:
                self.emit_sgu_loads(j)
            if nxt is not None:
                cbn = self.norm_pipeline(nxt)
            elif self.do_final:
                cbn = self.norm_pipeline(None)
            else:
                cbn = None
            if l % 2 == 0:
                self.emit_even_A(l, j, 0)
                self.emit_sgu_consts(j)
                if self._gate0_pending:
                    self.emit_mod_part(l, 16, 24, {})
                    self._gate0_pending = False
                if nxt is not None:
                    self.emit_mod_part(nxt, 0, 6, mst)
                self.emit_outproj(l, "ab", j, 0)
                self.emit_even_A(l, j, 4)
                vh = self.emit_sgu_stage1(j, 0)
                for bi in range(NT):
                    vh(bi)
                if nxt is not None:
                    self.emit_mod_part(nxt, 6, 12, mst)
                self.emit_outproj(l, "ab", j, 512)
                for cl in range(4):
                    vh = self.emit_sgu_stage1(j, cl + 1)
                    self.emit_sgu_stage2(j, cl, cl, vh)
                if nxt is not None:
                    self.emit_mod_part(nxt, 12, 18, mst)
                self.emit_outproj(l, "ab", j, 1024)
                for cl in range(4):
                    vh = self.emit_sgu_stage1(j, 4 + cl + 1) if 4 + cl + 1 < 8 else None
                    self.emit_sgu_stage2(j, 4 + cl, cl, vh)
                if nxt is not None:
                    self.emit_mod_part(nxt, 18, 24, mst)
                self.emit_outproj(l, "ab", j, 1536, cbn)
            else:
                self.emit_odd_layer(l, j, nxt, mst, cbn)
        if not layers and self.do_final:
            cbf = self.norm_pipeline(None)
            for tt in range(NT + 1):
                cbf(tt)
        if not self.do_final:
            self.emit_store_x()
        for s in self.outsem:
            if self.P.cnt.get(s, 0) > 0:
                self.P.wait("sp", (s, self.P.cnt[s]))

    def finish(self):
        nc, P = self.nc, self.P
        nblk = max(1, len(self.descs))
        self.wblk = nc.dram_tensor("wblk", [nblk, 128, 1024], F32, kind="ExternalInput").ap()
        for k in P.semkeys:
            P.sems[k] = self.es.enter_context(nc.semaphore(k))

        def replay(eng, e):
            for it in P.q[eng]:
                if it[0] == "wait":
                    e.wait_ge(P.sems[it[1]], it[2])
                elif it[0] == "dma_lazy":
                    ins = it[1](e)
                    if ins is not None:
                        ins.then_inc(P.sems[it[2]], it[3])
                else:
                    ins = it[1](e)
                    if it[2] is not None:
                        ins.then_inc(P.sems[it[2]], it[3])

        with nc.Block() as block:
            @block.sync
            def _(e):
                replay("sp", e)

            @block.scalar
            def _(e):
                replay("act", e)

            @block.vector
            def _(e):
                replay("dve", e)

            @block.gpsimd
            def _(e):
                replay("pool", e)

            @block.tensor
            def _(e):
                replay("pe", e)


def build(layers, do_final):
    nc = bass.Bass("TRN2", target_bir_lowering=False)
    with ExitStack() as es:
        em = Emitter(nc, es, layers, do_final)
        em.emit()
        em.finish()
    return nc, em.descs


def _colmajor(v):
    return np.ascontiguousarray(v.reshape(-1, 128).T)


def _make_block(desc, inp):
    kind = desc[0]
    if kind == "ada":
        _, l, mc = desc
        w = inp["ada_w"][l][:, mc * 128:(mc + 1) * 128]
        return w.reshape(8, 128, 128).transpose(1, 0, 2).reshape(128, 1024)
    if kind in ("win_e", "win_o"):
        _, j, col0 = desc
        src = inp["ab_w_in"] if kind == "win_e" else inp["c_w_in"]
        w = src[j][:, col0:col0 + 128]
        return w.reshape(8, 128, 128).transpose(1, 0, 2).reshape(128, 1024)
    if kind == "wout":
        _, arr, j, row0, mp = desc
        src = inp["ab_w_out"] if arr == "ab" else inp["c_w_out"]
        w = src[j][row0:row0 + 512, mp * 256:(mp + 1) * 256]
        return w.reshape(4, 128, 2, 128).transpose(1, 2, 0, 3).reshape(128, 1024)
    if kind == "poolw":
        _, j, g, mp = desc
        w = inp["c_pool_w"][j, g][:, mp * 256:(mp + 1) * 256]
        return w.reshape(4, 128, 2, 128).transpose(1, 2, 0, 3).reshape(128, 1024)
    raise ValueError(kind)


def _pack_vecs(inp):
    v = np.zeros((128, NV), np.float32)
    for l in range(DEPTH):
        v[:, V_NG + l * 8:V_NG + (l + 1) * 8] = _colmajor(inp["norm_g"][l])
        v[:, V_AB + l * 24:V_AB + (l + 1) * 24] = _colmajor(inp["ada_b"][l])
    v[:, V_FG:V_FG + 8] = _colmajor(inp["final_g"])
    for j in range(2):
        for tap in range(3):
            o = V_CW + (j * 3 + tap) * 8
            v[:, o:o + 8] = _colmajor(inp["ab_conv_w"][j, tap])
        v[:, V_LG + j * 8:V_LG + (j + 1) * 8] = _colmajor(inp["ab_ln_g"][j])
        v[:, V_LB + j * 8:V_LB + (j + 1) * 8] = _colmajor(inp["ab_ln_b"][j])
        v[:, V_PS + j * 16:V_PS + (j + 1) * 16] = _colmajor(inp["c_pool_scale"][j])
    return v


_CACHE = {}


def _get_prog(layers, do_final):
    key = (tuple(layers), do_final)
    if key not in _CACHE:
        _CACHE[key] = build(list(layers), do_final)
    return _CACHE[key]


def _run(layers, do_final, xT_list, inp):
    nc, descs = _get_prog(layers, do_final)
    if descs:
        wblk = np.empty((len(descs), 128, 1024), np.float32)
        for i, d in enumerate(descs):
            wblk[i] = _make_block(d, inp)
    else:
        wblk = np.zeros((1, 128, 1024), np.float32)
    vecs = _pack_vecs(inp)
    sgub = np.ascontiguousarray(inp["ab_sgu_b"].reshape(2, 1024))
    sguT = np.ascontiguousarray(inp["ab_sgu_w"].transpose(0, 3, 1, 2).reshape(2, 128, 1024))
    in_maps = []
    for b in range(8):
        in_maps.append({
            "xT": xT_list[b],
            "cvec": _colmajor(inp["c"][b]),
            "vecs": vecs,
            "sgub": sgub,
            "sguT": sguT,
            "wblk": wblk,
        })
    res = run_bass_kernel_spmd(nc, in_maps, core_ids=list(range(8)))
    return [np.asarray(r["outT"]) for r in res.results]


def kernel(**inputs):
    inp = {k: np.asarray(v, dtype=np.float32) for k, v in inputs.items()}
    x = inp["x"]
    xT = [np.ascontiguousarray(x[b].T) for b in range(8)]
    if FUSED:
        outT = _run([0, 1, 2, 3], True, xT, inp)
    else:
        for l in range(DEPTH):
            xT = _run([l], l == DEPTH - 1, xT, inp)
        outT = xT
    out = np.stack([np.ascontiguousarray(o.T) for o in outT], axis=0)
    return out.astype(np.float32)
```
